# Optimizing a Trainium2 kernel written in Bass

```python
import math
import jax, jax.numpy as jnp
from jax import lax
import numpy as np

D_MODEL = 2048
BATCH = 1
SEQ = 8192
DEPTH = 4
DEC_BATCH = 8
DEC_SEQ = 64
PAST_LEN = 1024

CHUNK = 64
N_MIXERS = 2
N_RET = (DEPTH + 1) // 2
N_SSM = DEPTH // 2
RET_HEADS = 8
RET_DK = D_MODEL // RET_HEADS
RET_DV = 2 * RET_DK
RET_QK = RET_HEADS * RET_DK
RET_VD = RET_HEADS * RET_DV
ROPE_BASE = 10000.0
SSM_GROUP = 16
SSM_GROUPS = D_MODEL // SSM_GROUP
SSM_STATE = 64
DT_MIN = 0.001
DT_MAX = 0.1
FFN_DIM = 2 * D_MODEL
CONV_W = 3
EPS = 1e-6

kernel_name = 'retnet_s5_streaming_hybrid_step'

F32 = jnp.float32


def rmsnorm(x, g):
    xf = x.astype(F32)
    y = xf * lax.rsqrt(jnp.mean(xf * xf, axis=-1, keepdims=True) + EPS)
    return (y * g.astype(F32)).astype(x.dtype)


def rotary(x, pos):
    half = RET_DK // 2
    inv = ROPE_BASE ** (-jnp.arange(half, dtype=F32) / half)
    ang = pos.astype(F32)[:, None] * inv[None, :]
    cos = jnp.cos(ang)[None, :, None, :]
    sin = jnp.sin(ang)[None, :, None, :]
    x1, x2 = x[..., :half], x[..., half:]
    return jnp.concatenate([x1 * cos - x2 * sin, x1 * sin + x2 * cos], axis=-1)


def retention_block(S, q, k, v, log_g):
    L = q.shape[1]
    idx = jnp.arange(L, dtype=F32)
    dist = jnp.abs(idx[:, None] - idx[None, :])
    intra = jnp.exp(dist[None] * log_g[:, None, None])
    scores = jnp.einsum('blhd,bmhd->bhlm', q, k) * intra[None]
    out = jnp.einsum('bhlm,bmhe->blhe', scores, v)
    cross = jnp.exp((idx + 1.0)[:, None] * log_g[None, :])
    out = out + jnp.einsum('blhd,bhde->blhe', q, S) * cross[None, :, :, None]
    k_dec = jnp.exp((L - 1.0 - idx)[:, None] * log_g[None, :])
    S_new = jnp.exp(L * log_g)[None, :, None, None] * S + jnp.einsum(
        'blhd,blhe->bhde', k * k_dec[None, :, :, None], v)
    return S_new, out


def to_blocks(t, nb, blk):
    B = t.shape[0]
    return jnp.moveaxis(t.reshape((B, nb, blk) + t.shape[2:]), 1, 0)


def from_blocks(t):
    t = jnp.moveaxis(t, 0, 1)
    return t.reshape((t.shape[0], t.shape[1] * t.shape[2]) + t.shape[3:])


def retention_mixer(h, S0, pos, w_in, gn, w_out):
    B, L, _ = h.shape
    proj = h @ w_in
    q, k, v, g = jnp.split(proj, [RET_QK, 2 * RET_QK, 2 * RET_QK + RET_VD], axis=-1)
    q = rotary(q.reshape(B, L, RET_HEADS, RET_DK).astype(F32), pos)
    k = rotary(k.reshape(B, L, RET_HEADS, RET_DK).astype(F32), pos) * (RET_DK ** -0.5)
    v = v.reshape(B, L, RET_HEADS, RET_DV).astype(F32)
    log_g = jnp.log1p(-jnp.exp2(-5.0 - jnp.arange(RET_HEADS, dtype=F32)))
    blk = min(L, CHUNK)
    nb = L // blk
    S_last, o = lax.scan(lambda S, xs: retention_block(S, xs[0], xs[1], xs[2], log_g),
                         S0.astype(F32), (to_blocks(q, nb, blk), to_blocks(k, nb, blk), to_blocks(v, nb, blk)))
    o = from_blocks(o)
    mu = jnp.mean(o, axis=-1, keepdims=True)
    var = jnp.mean(jnp.square(o - mu), axis=-1, keepdims=True)
    o = ((o - mu) * lax.rsqrt(var + EPS)).reshape(B, L, RET_VD) * gn.astype(F32)
    y = (o.astype(h.dtype) * jax.nn.silu(g)) @ w_out
    return y, S_last


def _ssm_combine(left, right):
    a1, b1 = left
    a2, b2 = right
    return a1 * a2, a2 * b1 + b2


def ssm_block(h0, u, A_bar, B_bar, C):
    bu = jnp.einsum('gpc,blgc->blgp', B_bar, u.astype(jnp.complex64))
    bu = bu.at[:, 0].add(A_bar[None] * h0)
    a = jnp.broadcast_to(A_bar, bu.shape)
    _, hs = lax.associative_scan(_ssm_combine, (a, bu), axis=1)
    y = jnp.einsum('gcp,blgp->blgc', C, hs).real
    return hs[:, -1], y


def ssm_mixer(h, h0, a_re, a_im, log_dt, b_re, b_im, c_re, c_im, d, w_glu):
    B, L, _ = h.shape
    A = lax.complex(a_re.astype(F32), a_im.astype(F32))
    dt = jnp.exp(log_dt.astype(F32))[:, None]
    A_bar = jnp.exp(A * dt)
    B_bar = ((A_bar - 1.0) / A)[..., None] * lax.complex(b_re.astype(F32), b_im.astype(F32))
    C = lax.complex(c_re.astype(F32), c_im.astype(F32))
    hf = h.astype(F32)
    u = hf.reshape(B, L, SSM_GROUPS, SSM_GROUP)
    blk = min(L, CHUNK)
    nb = L // blk
    h_last, y = lax.scan(lambda s, ub: ssm_block(s, ub, A_bar, B_bar, C), h0, to_blocks(u, nb, blk))
    y = from_blocks(y).reshape(B, L, D_MODEL) + d.astype(F32) * hf
    gl = jax.nn.gelu(y).astype(h.dtype)
    ga, gb = jnp.split(gl @ w_glu, 2, axis=-1)
    return ga * jax.nn.sigmoid(gb), h_last


def conv_ffn(h, prev, w_up, conv_w, conv_b, w_down):
    L = h.shape[1]
    a, b = jnp.split(h @ w_up, 2, axis=-1)
    a_pad = jnp.concatenate([prev.astype(a.dtype), a], axis=1)
    conv = conv_b
    for j in range(CONV_W):
        conv = conv + conv_w[j] * a_pad[:, j:j + L]
    y = (jax.nn.silu(conv) * b) @ w_down
    return y, a_pad[:, L:]


def setup_inputs(seed: int = 0) -> dict:
    key = jax.random.key(seed)
    ks = jax.random.split(key, 32)
    D, G, P = D_MODEL, SSM_GROUPS, SSM_STATE

    def nrm(k, shape, s):
        return jax.random.normal(k, shape, F32) * s

    return {
        'x_prompt': nrm(ks[0], (BATCH, SEQ, D), 1.0),
        'x_sample': nrm(ks[1], (DEC_BATCH, DEC_SEQ, D), 1.0),
        'state_ret': nrm(ks[2], (N_RET, DEC_BATCH, RET_HEADS, RET_DK, RET_DV), 0.05),
        'state_ssm_re': nrm(ks[3], (N_SSM, DEC_BATCH, G, P), 0.5),
        'state_ssm_im': nrm(ks[4], (N_SSM, DEC_BATCH, G, P), 0.5),
        'cache_conv': nrm(ks[5], (DEPTH, DEC_BATCH, CONV_W - 1, FFN_DIM), 1.0),
        'norm_mix': 1.0 + nrm(ks[6], (DEPTH, D), 0.01),
        'norm_ffn': 1.0 + nrm(ks[7], (DEPTH, D), 0.01),
        'norm_final': 1.0 + nrm(ks[8], (D,), 0.01),
        'ret_w_in': nrm(ks[9], (N_RET, D, 2 * RET_QK + 2 * RET_VD), D ** -0.5),
        'ret_gn': 1.0 + nrm(ks[10], (N_RET, RET_VD), 0.01),
        'ret_w_out': nrm(ks[11], (N_RET, RET_VD, D), RET_VD ** -0.5),
        'ssm_a_re': -0.5 + nrm(ks[12], (N_SSM, G, P), 0.01),
        'ssm_a_im': math.pi * jnp.arange(P, dtype=F32) + nrm(ks[13], (N_SSM, G, P), 0.01),
        'ssm_log_dt': jax.random.uniform(ks[14], (N_SSM, G), F32, math.log(DT_MIN), math.log(DT_MAX)),
        'ssm_b_re': nrm(ks[15], (N_SSM, G, P, SSM_GROUP), (2 * SSM_GROUP) ** -0.5),
        'ssm_b_im': nrm(ks[16], (N_SSM, G, P, SSM_GROUP), (2 * SSM_GROUP) ** -0.5),
        'ssm_c_re': nrm(ks[17], (N_SSM, G, SSM_GROUP, P), P ** -0.5),
        'ssm_c_im': nrm(ks[18], (N_SSM, G, SSM_GROUP, P), P ** -0.5),
        'ssm_d': nrm(ks[19], (N_SSM, D), 1.0),
        'ssm_w_glu': nrm(ks[20], (N_SSM, D, 2 * D), D ** -0.5),
        'ffn_w_up': nrm(ks[21], (DEPTH, D, 2 * FFN_DIM), D ** -0.5),
        'ffn_conv_w': nrm(ks[22], (DEPTH, CONV_W, FFN_DIM), CONV_W ** -0.5),
        'ffn_conv_b': nrm(ks[23], (DEPTH, FFN_DIM), 0.01),
        'ffn_w_down': nrm(ks[24], (DEPTH, FFN_DIM, D), FFN_DIM ** -0.5),
    }


def reference(x_prompt, x_sample, state_ret, state_ssm_re, state_ssm_im, cache_conv,
              norm_mix, norm_ffn, norm_final, ret_w_in, ret_gn, ret_w_out,
              ssm_a_re, ssm_a_im, ssm_log_dt, ssm_b_re, ssm_b_im, ssm_c_re, ssm_c_im, ssm_d, ssm_w_glu,
              ffn_w_up, ffn_conv_w, ffn_conv_b, ffn_w_down):

    def run_trunk(x, pos, s_ret, s_re, s_im, c_conv):
        new_ret, new_re, new_im, new_conv = [], [], [], []
        for i in range(DEPTH):
            j = i // N_MIXERS
            h = rmsnorm(x, norm_mix[i])
            if i % N_MIXERS == 0:
                y, s = retention_mixer(h, s_ret[j], pos, ret_w_in[j], ret_gn[j], ret_w_out[j])
                new_ret.append(s)
            else:
                h0 = lax.complex(s_re[j].astype(F32), s_im[j].astype(F32))
                y, s = ssm_mixer(h, h0, ssm_a_re[j], ssm_a_im[j], ssm_log_dt[j], ssm_b_re[j], ssm_b_im[j],
                                 ssm_c_re[j], ssm_c_im[j], ssm_d[j], ssm_w_glu[j])
                new_re.append(s.real)
                new_im.append(s.imag)
            x = x + y.astype(x.dtype)
            y, c = conv_ffn(rmsnorm(x, norm_ffn[i]), c_conv[i], ffn_w_up[i], ffn_conv_w[i],
                            ffn_conv_b[i], ffn_w_down[i])
            new_conv.append(c)
            x = x + y.astype(x.dtype)
        return (rmsnorm(x, norm_final), jnp.stack(new_ret), jnp.stack(new_re),
                jnp.stack(new_im), jnp.stack(new_conv))

    Bp, Lp, _ = x_prompt.shape
    pos_p = jnp.arange(Lp, dtype=jnp.int32)
    zr = jnp.zeros((N_RET, Bp, RET_HEADS, RET_DK, RET_DV), F32)
    zs = jnp.zeros((N_SSM, Bp, SSM_GROUPS, SSM_STATE), F32)
    zc = jnp.zeros((DEPTH, Bp, CONV_W - 1, FFN_DIM), x_prompt.dtype)
    y_prompt, ret_p, re_p, im_p, conv_p = run_trunk(x_prompt, pos_p, zr, zs, zs, zc)

    Ls = x_sample.shape[1]
    pos_s = PAST_LEN + jnp.arange(Ls, dtype=jnp.int32)
    y_sample, ret_s, re_s, im_s, conv_s = run_trunk(x_sample, pos_s, state_ret, state_ssm_re,
                                                    state_ssm_im, cache_conv)
    return (y_prompt, y_sample, ret_p, ret_s, re_p, im_p, re_s, im_s, conv_p, conv_s)
```

```python
import math
from contextlib import ExitStack
import numpy as np
import concourse.bass as bass
import concourse.mybir as mybir
from concourse.bass_utils import run_bass_kernel_spmd

F32 = mybir.dt.float32
F32R = mybir.dt.float32r
ALU = mybir.AluOpType
AF = mybir.ActivationFunctionType

D = 2048
KT = 16
DEPTH = 4
FF = 4096
FT = 32
SEQ = 8192
DEC_B = 8
DEC_S = 64
PAST = 1024
H = 8
DK = 256
DV = 512
EPS = 1e-6
TB = 256
NCORES = 8


def r_(ap):
    return ap.bitcast(F32R)


class Prog:
    def __init__(self):
        self.ops = []

    def add(self, eng, fn, r=(), w=(), dma=None):
        self.ops.append(dict(eng=eng, fn=fn, r=tuple(r), w=tuple(w), dma=dma))

    def emit(self, nc):
        ops = self.ops
        last_w = {}
        readers = {}
        for i, op in enumerate(ops):
            deps = set()
            for x in op['r']:
                if x in last_w:
                    deps.add(last_w[x])
            for x in op['w']:
                if x in last_w:
                    deps.add(last_w[x])
                deps |= readers.get(x, set())
            deps.discard(i)
            op['deps'] = {d for d in deps
                          if not (ops[d]['eng'] == 'pe' and op['eng'] == 'pe' and ops[d]['dma'] is None and op['dma'] is None)}
            for x in op['r']:
                readers.setdefault(x, set()).add(i)
            for x in op['w']:
                last_w[x] = i
                readers[x] = set()
        signal = [False] * len(ops)
        for op in ops:
            for d in op['deps']:
                signal[d] = True
        eng_cnt = {}
        dma_cnt = {}
        for i, op in enumerate(ops):
            if op['dma'] is not None:
                k = 'dma_' + op['dma']
                dma_cnt[k] = dma_cnt.get(k, 0) + 16
                op['sig'] = (k, dma_cnt[k])
            elif signal[i]:
                k = 'eng_' + op['eng']
                eng_cnt[k] = eng_cnt.get(k, 0) + 1
                op['sig'] = (k, eng_cnt[k])
            else:
                op['sig'] = None
        sem_names = sorted(set(list(eng_cnt) + list(dma_cnt)))
        with ExitStack() as st:
            sems = {k: st.enter_context(nc.semaphore(k)) for k in sem_names}
            block = st.enter_context(nc.Block())
            engs = {'sp': block.sync, 'act': block.scalar, 'dve': block.vector, 'pool': block.gpsimd, 'pe': block.tensor}

            def make(engname):
                mine = [op for op in ops if op['eng'] == engname]

                def body(e):
                    waited = {}
                    for op in mine:
                        need = {}
                        for d in op['deps']:
                            k, v = ops[d]['sig']
                            if v > need.get(k, 0):
                                need[k] = v
                        for k, v in need.items():
                            if waited.get(k, 0) < v:
                                e.wait_ge(sems[k], v)
                                waited[k] = v
                        ins = op['fn'](e)
                        if op['sig'] is not None:
                            k, v = op['sig']
                            ins.then_inc(sems[k], 16 if op['dma'] is not None else 1)
                    if engname == 'sp':
                        for k, v in dma_cnt.items():
                            e.wait_ge(sems[k], v)
                return body

            for name, deco in engs.items():
                deco(make(name))


MIXERS = 2


def build(nblk_p, layers=DEPTH):
    nc = bass.Bass("TRN2", target_bir_lowering=False)
    P = Prog()
    NTOK = nblk_p * TB + DEC_S

    def din(name, shape, dt=F32):
        return nc.dram_tensor(name, list(shape), dt, kind="ExternalInput").ap()

    def dout(name, shape):
        return nc.dram_tensor(name, list(shape), F32, kind="ExternalOutput").ap()

    xT_p = din("xT_p", [D, SEQ])
    xT_s = din("xT_s", [D, DEC_S])
    norms = din("norms", [128, 9, KT])
    w_up = din("w_up", [DEPTH, 64, 128, KT * 128], F32R)
    w_down = din("w_down", [DEPTH, KT, 2, 128, KT * 128], F32R)
    conv_w = din("conv_w", [128, DEPTH, 3, FT])
    conv_b = din("conv_b", [128, DEPTH, FT])
    cache_s = din("cache_s", [128, DEPTH, FT, 2])
    ones_d = din("ones_d", [128, 128], F32R)
    ident_d = din("ident_d", [128, 128])
    w_in = din("w_in", [2, 96, 128, KT * 128], F32R)
    w_v = din("w_v", [2, 16, 2, 128, 8 * 256], F32R)
    w_out = din("w_out", [2, KT, 2, 128, KT * 128], F32R)
    gn_d = din("gn_d", [128, 2, FT])
    sret_d = din("sret_d", [2, H, DK, DV], F32R)
    cosp_d = din("cosp_d", [128, SEQ])
    sinp_d = din("sinp_d", [128, SEQ])
    coss_d = din("coss_d", [128, DEC_S])
    sins_d = din("sins_d", [128, DEC_S])
    cross_d = din("cross_d", [128, H, TB])
    intra_d = din("intra_d", [128, H, 128])
    kdec_d = din("kdec_d", [128, H])
    sscr = nc.dram_tensor("sscr", [2, H, DK, DV], F32R).ap()
    bscr = nc.dram_tensor("bscr", [2, 16, 128, 1024], F32R).ap()
    cscr = nc.dram_tensor("cscr", [2, 16, 128, 1024], F32R).ap()
    w_glu = din("w_glu", [2, 32, 128, KT * 128], F32R)
    sm_d = din("sm_d", [128, 2, 3, 64])
    rc_d = din("rc_d", [128, 2, 5, 1024])
    csm_d = din("csm_d", [128, 2, 2, 1024])
    maskb_d = din("maskb_d", [128, 8])
    maska_d = din("maska_d", [128, 4])
    dsk_d = din("dsk_d", [128, 2, KT])
    h0_d = din("h0_d", [128, 2, 64, 2])
    ssm_p = dout("ssm_p", [2, 128, 64, 2])
    ssm_s = dout("ssm_s", [2, 128, 64, 2])
    ret_p = dout("ret_p", [2, H, DK, DV])
    ret_s = dout("ret_s", [2, H, DK, DV])

    yT = dout("yT", [D, NTOK])
    conv_p = dout("conv_p", [128, DEPTH, FT, 2])
    conv_s = dout("conv_s", [128, DEPTH, FT, 2])

    st = ExitStack()

    def sb(name, shape, dt=F32):
        return st.enter_context(nc.sbuf_tensor(name, list(shape), dt))

    XT = sb("XT", [128, KT, TB])
    HT = sb("HT", [128, KT, TB])
    BIG = sb("BIG", [128, FT, TB])
    NSLOT = 8
    SLOT = [sb(f"slot{i}", [128, 16 * 128]) for i in range(NSLOT)]
    NORM = sb("NORM", [128, 9, KT])
    CW = sb("CW", [128, DEPTH, 3, FT])
    CB = sb("CB", [128, DEPTH, FT])
    CC = sb("CC", [128, DEPTH, FT, 2])
    ONES = sb("ONES", [128, 128])
    SQ = [sb(f"SQ{i}", [128, TB]) for i in range(2)]
    STD = sb("STD", [128, TB])
    RSTD = sb("RSTD", [128, TB])
    AT = [sb(f"AT{i}", [128, TB + 2]) for i in range(2)]
    C0 = [sb(f"C0{i}", [128, TB]) for i in range(2)]
    C1 = [sb(f"C1{i}", [128, TB]) for i in range(2)]
    IDENT = sb("IDENT", [128, 128])
    GN = sb("GN", [128, 2, FT])
    CROSS = sb("CROSS", [128, H, TB])
    INTRA = sb("INTRA", [128, H, 128])
    KDEC = sb("KDEC", [128, H])
    COS = sb("COS", [128, TB])
    SIN = sb("SIN", [128, TB])
    RSC = sb("RSC", [128, 8192])
    _o = [0]

    def carve(n, pat, **kw):
        v = RSC[:, _o[0]:_o[0] + n].rearrange(pat, **kw)
        _o[0] += n
        return v
    QRAW = carve(2 * TB, "p (a t) -> p a t", t=TB)
    QT = carve(2 * TB, "p (a t) -> p a t", t=TB)
    QC = carve(2 * TB, "p (a t) -> p a t", t=TB)
    KTt = carve(2 * TB, "p (a t) -> p a t", t=TB)
    KTOK = carve(256, "p (a t) -> p a t", t=256)
    V = carve(512, "p (a t) -> p a t", t=512)
    SG = carve(4 * TB, "p (a t) -> p a t", t=TB)
    OH = carve(4 * TB, "p (a t) -> p a t", t=TB)
    OH2 = carve(4 * TB, "p (a t) -> p a t", t=TB)
    BU = RSC[:, 0:8192].rearrange("p (s r t) -> p s r t", r=2, t=64)
    TM = [sb(f"TM{i}", [128, TB]) for i in range(4)]
    SS = [carve(1024, "p (a t) -> p a t", t=512) for i in range(2)]
    ST = sb("ST", [128, 128])
    MEAN = sb("MEAN", [128, TB])
    MSQ = sb("MSQ", [128, TB])
    SMP = sb("SMP", [128, 2, 3, 64])
    MASKB = sb("MASKB", [128, 8])
    MASKA = sb("MASKA", [128, 4])
    DSK = sb("DSK", [128, 2, KT])
    HALFPI = sb("HALFPI", [128, 1])
    A2 = sb("A2", [128, 2, 64, 2])
    AIP = sb("AIP", [128, 2, 64])
    AIN = sb("AIN", [128, 2, 64])
    HHL = sb("HHL", [128, 2, 64, 2])
    TA = sb("TA", [128, 64, 2])
    TU = sb("TU", [128, 64, 2])
    STG = sb("STG", [128, 1024])
    NPS = 8
    PS = [st.enter_context(nc.psum_tensor(f"ps{i}", [128, 512], F32)) for i in range(NPS)]
    ps_rr = [0]

    def next_ps():
        i = ps_rr[0]
        ps_rr[0] = (i + 1) % NPS
        return i

    slot_rr = [0]

    def load_panel(src_ap, kts, ncols, rdep=()):
        s = slot_rr[0]
        slot_rr[0] = (s + 1) % NSLOT
        view = SLOT[s][:, 0:kts * ncols].rearrange("p (k n) -> p k n", n=ncols)
        flat = SLOT[s][:, 0:kts * ncols]
        P.add('pool', lambda e, flat=flat, src=src_ap: e.dma_start(out=r_(flat), in_=src),
              r=rdep, w=(f"slot{s}",), dma=f"slot{s}")
        return s, view

    P.add('sp', lambda e: e.dma_start(out=NORM[:], in_=norms), w=("NORM",), dma="c0")
    P.add('sp', lambda e: e.dma_start(out=CW[:], in_=conv_w), w=("CW",), dma="c1")
    P.add('sp', lambda e: e.dma_start(out=CB[:], in_=conv_b), w=("CB",), dma="c2")
    P.add('pool', lambda e: e.dma_start(out=r_(ONES[:]), in_=ones_d), w=("ONES",), dma="c3")
    P.add('sp', lambda e: e.dma_start(out=IDENT[:], in_=ident_d), w=("IDENT",), dma="c4")
    P.add('sp', lambda e: e.dma_start(out=GN[:], in_=gn_d), w=("GN",), dma="c5")
    P.add('sp', lambda e: e.dma_start(out=CROSS[:], in_=cross_d), w=("CROSS",), dma="c6")
    P.add('sp', lambda e: e.dma_start(out=INTRA[:], in_=intra_d), w=("INTRA",), dma="c7")
    P.add('sp', lambda e: e.dma_start(out=KDEC[:], in_=kdec_d), w=("KDEC",), dma="c8")

    def rmsnorm(T, nidx):
        pi = next_ps()
        for kt in range(KT):
            q = kt % 2
            P.add('act', lambda e, kt=kt, q=q: e.activation(out=r_(SQ[q][:, :T]), in_=XT[:, kt, :T], func=AF.Square),
                  r=(f"XT{kt}",), w=(f"SQ{q}",))
            P.add('pe', lambda e, kt=kt, q=q: e.matmul(PS[pi][:, :T], r_(ONES[:]), r_(SQ[q][:, :T]),
                                                          start=(kt == 0), stop=(kt == KT - 1)),
                  r=(f"SQ{q}", "ONES"), w=(f"ps{pi}",))
        P.add('act', lambda e: e.activation(out=STD[:, :T], in_=PS[pi][:, :T], func=AF.Sqrt, bias=EPS_AP[:, 0:1], scale=1.0 / D),
              r=(f"ps{pi}", "EPS"), w=("STD",))
        P.add('dve', lambda e: e.reciprocal(RSTD[:, :T], STD[:, :T]), r=("STD",), w=("RSTD",))
        for kt in range(KT):
            P.add('dve', lambda e, kt=kt: e.scalar_tensor_tensor(out=r_(HT[:, kt, :T]), in0=XT[:, kt, :T],
                                                                   scalar=NORM[:, nidx, kt:kt + 1], in1=RSTD[:, :T],
                                                                   op0=ALU.mult, op1=ALU.mult),
                  r=(f"XT{kt}", "NORM", "RSTD"), w=(f"HT{kt}",))

    EPS_AP = sb("EPS_AP", [128, 1])
    P.add('dve', lambda e: e.memset(EPS_AP[:], EPS), w=("EPS",))

    def proj_fm(src_w, col0, kts, srcname, SRC, T):
        s, view = load_panel(src_w[col0 // 128], kts, 128)
        pi = next_ps()
        for kt in range(kts):
            P.add('pe', lambda e, kt=kt, view=view: e.matmul(PS[pi][:, :T], r_(view[:, kt, :]), r_(SRC[:, kt, :T]),
                                                               start=(kt == 0), stop=(kt == kts - 1)),
                  r=(f"slot{s}", f"{srcname}{kt}"), w=(f"ps{pi}",))
        return pi

    def ffn(T, li, kind):
        rmsnorm(T, 4 + li)
        for j in range(FT):
            q = j % 2
            pa = proj_fm(w_up[li], j * 128, KT, "HT", HT, T)
            pb = proj_fm(w_up[li], FF + j * 128, KT, "HT", HT, T)
            ccn = f"CC{li}_{j}"
            P.add('act', lambda e, q=q, j=j: e.activation(out=AT[q][:, 0:2], in_=CC[:, li, j, :], func=AF.Copy),
                  r=(ccn,), w=(f"AT{q}",))
            P.add('act', lambda e, q=q, pa=pa: e.activation(out=AT[q][:, 2:2 + T], in_=PS[pa][:, :T], func=AF.Copy),
                  r=(f"ps{pa}",), w=(f"AT{q}",))
            P.add('act', lambda e, q=q, j=j: e.activation(out=C0[q][:, :T], in_=AT[q][:, 0:T], func=AF.Identity,
                                                            bias=CB[:, li, j:j + 1], scale=CW[:, li, 0, j:j + 1]),
                  r=(f"AT{q}", "CW", "CB"), w=(f"C0{q}",))
            P.add('dve', lambda e, q=q, j=j: e.scalar_tensor_tensor(out=C1[q][:, :T], in0=AT[q][:, 1:T + 1],
                                                                      scalar=CW[:, li, 1, j:j + 1], in1=C0[q][:, :T],
                                                                      op0=ALU.mult, op1=ALU.add),
                  r=(f"AT{q}", f"C0{q}", "CW"), w=(f"C1{q}",))
            P.add('dve', lambda e, q=q, j=j: e.scalar_tensor_tensor(out=C0[q][:, :T], in0=AT[q][:, 2:T + 2],
                                                                      scalar=CW[:, li, 2, j:j + 1], in1=C1[q][:, :T],
                                                                      op0=ALU.mult, op1=ALU.add),
                  r=(f"AT{q}", f"C1{q}", "CW"), w=(f"C0{q}",))
            P.add('act', lambda e, q=q, j=j: e.activation(out=CC[:, li, j, :], in_=AT[q][:, T:T + 2], func=AF.Copy),
                  r=(f"AT{q}",), w=(ccn,))
            P.add('act', lambda e, q=q: e.activation(out=C1[q][:, :T], in_=C0[q][:, :T], func=AF.Silu),
                  r=(f"C0{q}",), w=(f"C1{q}",))
            P.add('dve', lambda e, q=q, j=j, pb=pb: e.tensor_tensor(out=r_(BIG[:, j, :T]), in0=C1[q][:, :T], in1=PS[pb][:, :T], op=ALU.mult),
                  r=(f"C1{q}", f"ps{pb}"), w=(f"BIG{j}",))
        proj_big_add(w_down[li], T)

    def proj_big_add(w, T):
        for m in range(KT):
            pi = next_ps()
            for hf in range(2):
                s_, view = load_panel(w[m, hf], KT, 128)
                for k8 in range(KT):
                    kt = hf * KT + k8
                    P.add('pe', lambda e, k8=k8, kt=kt, view=view, pi=pi: e.matmul(PS[pi][:, :T], r_(view[:, k8, :]), r_(BIG[:, kt, :T]),
                                                                                  start=(kt == 0), stop=(kt == FT - 1)),
                          r=(f"slot{s_}", f"BIG{kt}"), w=(f"ps{pi}",))
            P.add('dve', lambda e, m=m, pi=pi: e.tensor_tensor(out=XT[:, m, :T], in0=XT[:, m, :T], in1=PS[pi][:, :T], op=ALU.add),
                  r=(f"XT{m}", f"ps{pi}"), w=(f"XT{m}",))

    GAM = [1.0 - 2.0 ** (-5.0 - h) for h in range(H)]

    def rotary(dst, dstname, T):
        X1, X2 = QRAW[:, 0, :T], QRAW[:, 1, :T]
        P.add('dve', lambda e: e.tensor_tensor(out=TM[0][:, :T], in0=X1, in1=COS[:, :T], op=ALU.mult), r=("QRAW", "COS"), w=("TM0",))
        P.add('dve', lambda e: e.tensor_tensor(out=TM[1][:, :T], in0=X2, in1=SIN[:, :T], op=ALU.mult), r=("QRAW", "SIN"), w=("TM1",))
        P.add('dve', lambda e: e.tensor_tensor(out=r_(dst[:, 0, :T]), in0=TM[0][:, :T], in1=TM[1][:, :T], op=ALU.subtract),
              r=("TM0", "TM1"), w=(dstname,))
        P.add('dve', lambda e: e.tensor_tensor(out=TM[2][:, :T], in0=X1, in1=SIN[:, :T], op=ALU.mult), r=("QRAW", "SIN"), w=("TM2",))
        P.add('dve', lambda e: e.tensor_tensor(out=TM[3][:, :T], in0=X2, in1=COS[:, :T], op=ALU.mult), r=("QRAW", "COS"), w=("TM3",))
        P.add('dve', lambda e: e.tensor_tensor(out=r_(dst[:, 1, :T]), in0=TM[2][:, :T], in1=TM[3][:, :T], op=ALU.add),
              r=("TM2", "TM3"), w=(dstname,))

    def retention(T, li, kind, first, last, t0):
        j = li // 2
        TT = min(128, T)
        ntile = T // TT
        nch = TT // 64
        wi = w_in[j]
        rmsnorm(T, li)
        cs, sn = (cosp_d, sinp_d) if kind == 'p' else (coss_d, sins_d)
        P.add('sp', lambda e: e.dma_start(out=COS[:, :T], in_=cs[:, t0:t0 + T]), w=("COS",), dma="cos")
        P.add('sp', lambda e: e.dma_start(out=SIN[:, :T], in_=sn[:, t0:t0 + T]), w=("SIN",), dma="sin")
        for h in range(H):
            for dt in range(2):
                pi = proj_fm(wi, h * 256 + dt * 128, KT, "HT", HT, T)
                P.add('act', lambda e, dt=dt, pi=pi: e.activation(out=r_(QRAW[:, dt, :T]), in_=PS[pi][:, :T], func=AF.Copy),
                      r=(f"ps{pi}",), w=("QRAW",))
            rotary(QT, "QT", T)
            for dt in range(2):
                P.add('dve', lambda e, dt=dt, h=h: e.tensor_tensor(out=r_(QC[:, dt, :T]), in0=QT[:, dt, :T], in1=CROSS[:, h, :T], op=ALU.mult),
                      r=("QT", "CROSS"), w=("QC",))
            for dt in range(2):
                pi = proj_fm(wi, 2048 + h * 256 + dt * 128, KT, "HT", HT, T)
                P.add('act', lambda e, dt=dt, pi=pi: e.activation(out=r_(QRAW[:, dt, :T]), in_=PS[pi][:, :T], func=AF.Copy),
                      r=(f"ps{pi}",), w=("QRAW",))
            rotary(KTt, "KTt", T)
            for jj in range(4):
                pi = proj_fm(wi, 8192 + h * 512 + jj * 128, KT, "HT", HT, T)
                P.add('act', lambda e, jj=jj, pi=pi: e.activation(out=r_(SG[:, jj, :T]), in_=PS[pi][:, :T], func=AF.Silu),
                      r=(f"ps{pi}",), w=(f"SG{jj}",))
            if first and kind == 'p':
                P.add('dve', lambda e: e.tensor_scalar(out=r_(SS[0].rearrange("p a t -> p (a t)")), in0=IDENT[:, 0:1].broadcast_to([128, 1024]), scalar1=0.0, scalar2=None, op0=ALU.mult),
                      r=("IDENT",), w=("SS0",))
            else:
                srcS = (sret_d[j, h] if first else sscr[j, h]).rearrange("(dt p) e -> p dt e", p=128)
                P.add('pool', lambda e, srcS=srcS: e.dma_start(out=r_(SS[0][:]), in_=srcS), r=("sscr",) if not first else (), w=("SS0",), dma="sld")
            cur = 0
            for tt in range(ntile):
                tok0 = tt * TT
                for half in range(2):
                    pi = next_ps()
                    for kh in range(2):
                        s_, view = load_panel(w_v[j][h * 2 + half, kh], 8, 256)
                        for k8 in range(8):
                            kt = kh * 8 + k8
                            P.add('pe', lambda e, k8=k8, kt=kt, view=view, pi=pi, tok0=tok0: e.matmul(PS[pi][:TT, 0:256], r_(HT[:, kt, tok0:tok0 + TT]), r_(view[:, k8, :]),
                                                                                                    start=(kt == 0), stop=(kt == KT - 1)),
                                  r=(f"slot{s_}", f"HT{kt}"), w=(f"ps{pi}",))
                    P.add('act', lambda e, half=half, pi=pi: e.activation(out=r_(V[:TT, 0, half * 256:(half + 1) * 256]), in_=PS[pi][:TT, 0:256], func=AF.Copy),
                          r=(f"ps{pi}",), w=("V",))
                for dt in range(2):
                    pi = next_ps()
                    P.add('pe', lambda e, dt=dt, pi=pi, tok0=tok0: e.transpose(PS[pi][:TT, 0:128], KTt[:, dt, tok0:tok0 + TT], IDENT[:]),
                          r=("KTt", "IDENT"), w=(f"ps{pi}",))
                    P.add('dve', lambda e, dt=dt, pi=pi, h=h: e.tensor_scalar(out=r_(KTOK[:TT, 0, dt * 128:(dt + 1) * 128]), in0=PS[pi][:TT, 0:128],
                                                                               scalar1=KDEC[:TT, h:h + 1], scalar2=None, op0=ALU.mult),
                          r=(f"ps{pi}", "KDEC"), w=("KTOK",))
                pi = next_ps()
                for dt in range(2):
                    P.add('pe', lambda e, dt=dt, pi=pi, tok0=tok0: e.matmul(PS[pi][:TT, :TT], r_(KTt[:, dt, tok0:tok0 + TT]), r_(QT[:, dt, tok0:tok0 + TT]),
                                                                              start=(dt == 0), stop=(dt == 1)),
                          r=("KTt", "QT"), w=(f"ps{pi}",))
                P.add('dve', lambda e, pi=pi, h=h: e.tensor_tensor(out=r_(ST[:TT, :TT]), in0=PS[pi][:TT, :TT], in1=INTRA[:TT, h, :TT], op=ALU.mult),
                      r=(f"ps{pi}", "INTRA"), w=("ST",))
                for cc in range(nch):
                    p0 = cc * 64
                    c0 = tok0 + cc * 64
                    for dt in range(2):
                        pi = next_ps()
                        P.add('pe', lambda e, dt=dt, pi=pi, p0=p0: e.matmul(PS[pi][:, 0:512], r_(KTOK[p0:p0 + 64, 0, dt * 128:(dt + 1) * 128]), r_(V[p0:p0 + 64, 0, :]),
                                                                              start=True, stop=True),
                              r=("KTOK", "V"), w=(f"ps{pi}",))
                        P.add('dve', lambda e, dt=dt, pi=pi, h=h, cur=cur: e.scalar_tensor_tensor(out=r_(SS[1 - cur][:, dt, :]), in0=SS[cur][:, dt, :], scalar=GAM[h] ** 64,
                                                                                                   in1=PS[pi][:, 0:512], op0=ALU.mult, op1=ALU.add),
                              r=(f"ps{pi}", f"SS{cur}"), w=(f"SS{1 - cur}",))
                    po = next_ps()
                    for jj in range(4):
                        P.add('pe', lambda e, jj=jj, po=po, p0=p0, cc=cc: e.matmul(PS[po][:, jj * 64:(jj + 1) * 64], r_(V[p0:p0 + 64, 0, jj * 128:(jj + 1) * 128]),
                                                                                     r_(ST[p0:p0 + 64, cc * 64:(cc + 1) * 64]), start=True, stop=False),
                              r=("V", "ST"), w=(f"ps{po}",))
                        for dt in range(2):
                            P.add('pe', lambda e, jj=jj, po=po, dt=dt, c0=c0, cur=cur: e.matmul(PS[po][:, jj * 64:(jj + 1) * 64], r_(SS[cur][:, dt, jj * 128:(jj + 1) * 128]),
                                                                                                  r_(QC[:, dt, c0:c0 + 64]), start=False, stop=(dt == 1)),
                                  r=(f"SS{cur}", "QC"), w=(f"ps{po}",))
                    P.add('act', lambda e, po=po, c0=c0: e.activation(out=r_(OH[:, :, c0:c0 + 64]), in_=PS[po][:, 0:256].rearrange("p (j l) -> p j l", l=64), func=AF.Copy),
                          r=(f"ps{po}",), w=("OH",))
                    cur = 1 - cur
            if last:
                dst = (ret_p if kind == 'p' else ret_s)[j, h].rearrange("(dt p) e -> p dt e", p=128)
                P.add('sp', lambda e, dst=dst, cur=cur: e.dma_start(out=dst, in_=SS[cur][:]), r=(f"SS{cur}",), dma="sst")
            else:
                dst = sscr[j, h].rearrange("(dt p) e -> p dt e", p=128)
                P.add('pool', lambda e, dst=dst, cur=cur: e.dma_start(out=dst, in_=r_(SS[cur][:])), r=(f"SS{cur}",), w=("sscr",), dma="sst2")
            p1 = next_ps()
            for jj in range(4):
                P.add('pe', lambda e, jj=jj, p1=p1: e.matmul(PS[p1][:, :T], r_(ONES[:]), r_(OH[:, jj, :T]), start=(jj == 0), stop=(jj == 3)),
                      r=("OH", "ONES"), w=(f"ps{p1}",))
            P.add('act', lambda e: e.activation(out=r_(OH2[:, :, :T]), in_=OH[:, :, :T], func=AF.Square), r=("OH",), w=("OH2",))
            p2 = next_ps()
            for jj in range(4):
                P.add('pe', lambda e, jj=jj, p2=p2: e.matmul(PS[p2][:, :T], r_(ONES[:]), r_(OH2[:, jj, :T]), start=(jj == 0), stop=(jj == 3)),
                      r=("OH2", "ONES"), w=(f"ps{p2}",))
            P.add('act', lambda e, p1=p1: e.activation(out=MEAN[:, :T], in_=PS[p1][:, :T], func=AF.Copy, scale=1.0 / DV), r=(f"ps{p1}",), w=("MEAN",))
            P.add('dve', lambda e: e.tensor_tensor(out=MSQ[:, :T], in0=MEAN[:, :T], in1=MEAN[:, :T], op=ALU.mult), r=("MEAN",), w=("MSQ",))
            P.add('dve', lambda e, p2=p2: e.scalar_tensor_tensor(out=MSQ[:, :T], in0=PS[p2][:, :T], scalar=1.0 / DV, in1=MSQ[:, :T], op0=ALU.mult, op1=ALU.subtract),
                  r=(f"ps{p2}", "MSQ"), w=("MSQ",))
            P.add('act', lambda e: e.activation(out=STD[:, :T], in_=MSQ[:, :T], func=AF.Sqrt, bias=EPS_AP[:, 0:1], scale=1.0), r=("MSQ", "EPS"), w=("STD",))
            P.add('dve', lambda e: e.reciprocal(RSTD[:, :T], STD[:, :T]), r=("STD",), w=("RSTD",))
            for jj in range(4):
                q = jj % 2
                P.add('dve', lambda e, jj=jj, q=q: e.tensor_tensor(out=TM[q][:, :T], in0=OH[:, jj, :T], in1=MEAN[:, :T], op=ALU.subtract),
                      r=("OH", "MEAN"), w=(f"TM{q}",))
                P.add('dve', lambda e, jj=jj, q=q: e.tensor_tensor(out=TM[q + 2][:, :T], in0=TM[q][:, :T], in1=RSTD[:, :T], op=ALU.mult),
                      r=(f"TM{q}", "RSTD"), w=(f"TM{q + 2}",))
                P.add('dve', lambda e, jj=jj, q=q, h=h: e.scalar_tensor_tensor(out=r_(BIG[:, h * 4 + jj, :T]), in0=TM[q + 2][:, :T], scalar=GN[:, j, h * 4 + jj:h * 4 + jj + 1],
                                                                                in1=SG[:, jj, :T], op0=ALU.mult, op1=ALU.mult),
                      r=(f"TM{q + 2}", "GN", f"SG{jj}"), w=(f"BIG{h * 4 + jj}",))
        proj_big_add(w_out[j], T)

    P.add('sp', lambda e: e.dma_start(out=SMP[:], in_=sm_d), w=("SMP",), dma="c9")
    P.add('sp', lambda e: e.dma_start(out=MASKB[:], in_=maskb_d), w=("MASKB",), dma="c10")
    P.add('sp', lambda e: e.dma_start(out=MASKA[:], in_=maska_d), w=("MASKA",), dma="c11")
    P.add('sp', lambda e: e.dma_start(out=DSK[:], in_=dsk_d), w=("DSK",), dma="c12")
    P.add('dve', lambda e: e.memset(HALFPI[:], math.pi / 2), w=("HALFPI",))

    def tt(out, a, b, op, rn, wn):
        P.add('dve', lambda e: e.tensor_tensor(out=out, in0=a, in1=b, op=op), r=rn, w=wn)

    def abar(ARE, AIM, LDT, W, names):
        DT, MAG, TH, ZR, ZI, T1 = W
        n = names
        P.add('act', lambda e: e.activation(out=DT, in_=LDT, func=AF.Exp), r=(n + "in",), w=(n + "DT",))
        P.add('dve', lambda e: e.scalar_tensor_tensor(out=MAG, in0=ARE, scalar=1.0 / 32, in1=DT, op0=ALU.mult, op1=ALU.mult), r=(n + "in", n + "DT"), w=(n + "MAG",))
        P.add('act', lambda e: e.activation(out=MAG, in_=MAG, func=AF.Exp), r=(n + "MAG",), w=(n + "MAG",))
        P.add('dve', lambda e: e.scalar_tensor_tensor(out=TH, in0=AIM, scalar=1.0 / 32, in1=DT, op0=ALU.mult, op1=ALU.mult), r=(n + "in", n + "DT"), w=(n + "TH",))
        P.add('act', lambda e: e.activation(out=ZI, in_=TH, func=AF.Sin), r=(n + "TH",), w=(n + "ZI",))
        P.add('act', lambda e: e.activation(out=ZR, in_=TH, func=AF.Sin, bias=HALFPI[:, 0:1], scale=1.0), r=(n + "TH", "HALFPI"), w=(n + "ZR",))
        tt(ZR, ZR, MAG, ALU.mult, (n + "ZR", n + "MAG"), (n + "ZR",))
        tt(ZI, ZI, MAG, ALU.mult, (n + "ZI", n + "MAG"), (n + "ZI",))
        for _ in range(5):
            tt(T1, ZR, ZR, ALU.mult, (n + "ZR",), (n + "T1",))
            tt(TH, ZI, ZI, ALU.mult, (n + "ZI",), (n + "TH",))
            P.add('dve', lambda e: e.scalar_tensor_tensor(out=ZI, in0=ZR, scalar=2.0, in1=ZI, op0=ALU.mult, op1=ALU.mult), r=(n + "ZR", n + "ZI"), w=(n + "ZI",))
            tt(ZR, T1, TH, ALU.subtract, (n + "T1", n + "TH"), (n + "ZR",))
        return ZR, ZI

    def ssm_prep(j):
        orig_add = P.add

        def chained(eng, fn, r=(), w=(), dma=None):
            orig_add(eng, fn, r=tuple(r) + ("prep",), w=tuple(w) + ("prep",), dma=dma)
        P.add = chained
        try:
            Xt = [XT[:, k, :] for k in range(KT)]
            P.add('dve', lambda e: e.tensor_copy(STG[:, 1000:1001], SMP[:, 0, 0, 0:1]), r=("SMP", "MASKB", "MASKA", "HALFPI"))
            sm = SMP[:, j]
            Wt = [STG[:, 64 * i:64 * (i + 1)] for i in range(6)]
            ZR, ZI = abar(sm[:, 0, :], sm[:, 1, :], sm[:, 2, :], Wt, "p")
            P.add('dve', lambda e, ZR=ZR: e.tensor_copy(A2[:, j, :, 0], ZR))
            P.add('dve', lambda e, ZR=ZR: e.tensor_copy(A2[:, j, :, 1], ZR))
            P.add('dve', lambda e, ZI=ZI: e.tensor_copy(AIP[:, j, :], ZI))
            P.add('dve', lambda e, ZI=ZI: e.tensor_scalar(out=AIN[:, j, :], in0=ZI, scalar1=-1.0, scalar2=None, op0=ALU.mult))
            for Q in range(4):
                ARE, AIM, LDT, BRE, BIM = Xt[0:5]
                for i, dst in enumerate([ARE, AIM, LDT, BRE, BIM]):
                    P.add('sp', lambda e, i=i, dst=dst, Q=Q: e.dma_start(out=dst, in_=rc_d[:, j, i, 256 * Q:256 * (Q + 1)]), dma="rcld")
                ZR, ZI = abar(ARE, AIM, LDT, Xt[5:11], "p")
                T1, T2, INV, AM1, CR, CI = Xt[5], Xt[6], Xt[7], Xt[10], Xt[11], Xt[12]
                n = ()
                tt(T1, ARE, ARE, ALU.mult, n, n)
                tt(T2, AIM, AIM, ALU.mult, n, n)
                tt(T1, T1, T2, ALU.add, n, n)
                P.add('dve', lambda e, INV=INV, T1=T1: e.reciprocal(INV, T1))
                P.add('dve', lambda e, AM1=AM1, ZR=ZR: e.tensor_scalar(out=AM1, in0=ZR, scalar1=-1.0, scalar2=None, op0=ALU.add))
                tt(T1, AM1, ARE, ALU.mult, n, n)
                tt(T2, ZI, AIM, ALU.mult, n, n)
                tt(T1, T1, T2, ALU.add, n, n)
                tt(CR, T1, INV, ALU.mult, n, n)
                tt(T1, ZI, ARE, ALU.mult, n, n)
                tt(T2, AM1, AIM, ALU.mult, n, n)
                tt(T1, T1, T2, ALU.subtract, n, n)
                tt(CI, T1, INV, ALU.mult, n, n)
                BBR, BBI = Xt[13], Xt[14]
                tt(T1, CR, BRE, ALU.mult, n, n)
                tt(T2, CI, BIM, ALU.mult, n, n)
                tt(BBR, T1, T2, ALU.subtract, n, n)
                tt(T1, CR, BIM, ALU.mult, n, n)
                tt(T2, CI, BRE, ALU.mult, n, n)
                tt(BBI, T1, T2, ALU.add, n, n)
                BB = [BBR.rearrange("p (g x) -> p g x", x=64), BBI.rearrange("p (g x) -> p g x", x=64)]
                for gl_ in range(4):
                    gt = 4 * Q + gl_
                    stg = STG[:].rearrange("p (q r a x) -> p q r a x", q=4, r=2, a=2)
                    for q in range(4):
                        for ri in range(2):
                            for a_ in range(2):
                                P.add('dve', lambda e, gl_=gl_, q=q, ri=ri, a_=a_, stg=stg, BB=BB: e.tensor_scalar(
                                    out=stg[:, q, ri, a_, :], in0=BB[ri][:, gl_, :], scalar1=MASKB[:, 2 * q + a_:2 * q + a_ + 1], scalar2=None, op0=ALU.mult))
                    P.add('pool', lambda e, gt=gt: e.dma_start(out=bscr[j, gt], in_=r_(STG[:])), dma="stg")
                CRE, CIM = Xt[0], Xt[1]
                for i, dst in enumerate([CRE, CIM]):
                    P.add('sp', lambda e, i=i, dst=dst, Q=Q: e.dma_start(out=dst, in_=csm_d[:, j, i, 256 * Q:256 * (Q + 1)]), dma="rcld")
                CS = [CRE.rearrange("p (s c) -> p s c", c=16), CIM.rearrange("p (s c) -> p s c", c=16)]
                for gl_ in range(4):
                    gt = 4 * Q + gl_
                    stg = STG[:].rearrange("p (q r x) -> p q r x", q=4, r=2)
                    P.add('dve', lambda e: e.memset(STG[:], 0.0))
                    for q in range(4):
                        for ri in range(2):
                            for a_ in range(2):
                                c0 = 32 * q + 16 * a_
                                P.add('dve', lambda e, gl_=gl_, q=q, ri=ri, a_=a_, c0=c0, stg=stg, CS=CS: e.tensor_scalar(
                                    out=stg[:, q, ri, c0:c0 + 16], in0=CS[ri][:, gl_ * 4 + q, :], scalar1=MASKA[:, 2 * ri + a_:2 * ri + a_ + 1], scalar2=None, op0=ALU.mult))
                    P.add('pool', lambda e, gt=gt: e.dma_start(out=cscr[j, gt], in_=r_(STG[:])), dma="stg")
        finally:
            del P.add

    def ssm(T, li, kind, first, last):
        j = li // 2
        rmsnorm(T, li)
        HH = HHL[:, j]
        hn = f"HH{j}"
        if first:
            if kind == 'p':
                P.add('dve', lambda e: e.memset(HH, 0.0), w=(hn,))
            else:
                P.add('sp', lambda e: e.dma_start(out=HH, in_=h0_d[:, j]), w=(hn,), dma="hld")
        bun = tuple(f"BU{t}" for t in range(64))
        for sub in range(T // 64):
            c0 = sub * 64
            for gt in range(16):
                s_, view = load_panel(bscr[j, gt], 1, 1024, rdep=("bscr",))
                vw = view[:, 0, :].rearrange("p (k n) -> p k n", n=128)
                pi = next_ps()
                for k in range(8):
                    P.add('pe', lambda e, k=k, vw=vw, pi=pi, gt=gt, c0=c0: e.matmul(PS[pi][:, k * 64:(k + 1) * 64], r_(vw[:, k, :]), r_(HT[:, gt, c0:c0 + 64]), start=True, stop=True),
                          r=(f"slot{s_}", f"HT{gt}", "bscr"), w=(f"ps{pi}",))
                P.add('act', lambda e, pi=pi, gt=gt: e.activation(out=r_(BU[:, gt * 4:(gt + 1) * 4, :, :]), in_=PS[pi][:, :].rearrange("p (q r t) -> p q r t", q=4, r=2), func=AF.Copy),
                      r=(f"ps{pi}",), w=bun)
            for t in range(64):
                prev = HH if t == 0 else BU[:, :, :, t - 1]
                pn = hn if t == 0 else f"BU{t - 1}"
                cur = BU[:, :, :, t]
                P.add('dve', lambda e, prev=prev: e.tensor_tensor(out=TA[:], in0=A2[:, j], in1=prev, op=ALU.mult), r=(pn, f"A2{j}"), w=("TA",))
                P.add('dve', lambda e, prev=prev: e.tensor_tensor(out=TU[:, :, 0], in0=AIN[:, j, :], in1=prev[:, :, 1], op=ALU.mult), r=(pn, f"AI{j}"), w=("TU",))
                P.add('dve', lambda e, prev=prev: e.tensor_tensor(out=TU[:, :, 1], in0=AIP[:, j, :], in1=prev[:, :, 0], op=ALU.mult), r=(pn, f"AI{j}"), w=("TU",))
                P.add('dve', lambda e, cur=cur: e.tensor_tensor(out=r_(cur), in0=cur, in1=TA[:], op=ALU.add), r=(f"BU{t}", "TA"), w=(f"BU{t}",))
                P.add('dve', lambda e, cur=cur: e.tensor_tensor(out=r_(cur), in0=cur, in1=TU[:], op=ALU.add), r=(f"BU{t}", "TU"), w=(f"BU{t}",))
            P.add('dve', lambda e: e.tensor_copy(HH, BU[:, :, :, 63]), r=("BU63",), w=(hn,))
            for gt in range(16):
                s_, view = load_panel(cscr[j, gt], 1, 1024, rdep=("cscr",))
                vw = view[:, 0, :].rearrange("p (k n) -> p k n", n=128)
                pi = next_ps()
                for k in range(8):
                    P.add('pe', lambda e, k=k, vw=vw, pi=pi, gt=gt: e.matmul(PS[pi][:, 0:64], r_(vw[:, k, :]), r_(BU[:, gt * 4 + k // 2, k % 2, :]), start=(k == 0), stop=(k == 7)),
                          r=(f"slot{s_}", "cscr") + bun, w=(f"ps{pi}",))
                P.add('dve', lambda e, pi=pi, gt=gt, c0=c0: e.scalar_tensor_tensor(out=r_(BIG[:, gt, c0:c0 + 64]), in0=HT[:, gt, c0:c0 + 64], scalar=DSK[:, j, gt:gt + 1], in1=PS[pi][:, 0:64],
                                                                                    op0=ALU.mult, op1=ALU.add),
                      r=(f"ps{pi}", f"HT{gt}", "DSK"), w=(f"BIG{gt}",))
        if last:
            dst = (ssm_p if kind == 'p' else ssm_s)[j]
            P.add('sp', lambda e: e.dma_start(out=dst, in_=HH), r=(hn,), dma="hst")
        for gt in range(16):
            Y = BIG[:, gt, :T]
            q = gt % 2
            A_, B_ = TM[q], TM[q + 2]
            P.add('dve', lambda e, Y=Y, A_=A_: e.tensor_tensor(out=A_[:, :T], in0=Y, in1=Y, op=ALU.mult), r=(f"BIG{gt}",), w=(f"TM{q}",))
            P.add('dve', lambda e, A_=A_: e.tensor_scalar(out=A_[:, :T], in0=A_[:, :T], scalar1=0.044715, scalar2=1.0, op0=ALU.mult, op1=ALU.add), r=(f"TM{q}",), w=(f"TM{q}",))
            P.add('dve', lambda e, Y=Y, A_=A_: e.tensor_tensor(out=A_[:, :T], in0=A_[:, :T], in1=Y, op=ALU.mult), r=(f"TM{q}", f"BIG{gt}"), w=(f"TM{q}",))
            P.add('act', lambda e, A_=A_, B_=B_: e.activation(out=B_[:, :T], in_=A_[:, :T], func=AF.Tanh, scale=math.sqrt(2.0 / math.pi)), r=(f"TM{q}",), w=(f"TM{q + 2}",))
            P.add('dve', lambda e, Y=Y, B_=B_, A_=A_: e.scalar_tensor_tensor(out=A_[:, :T], in0=B_[:, :T], scalar=1.0, in1=Y, op0=ALU.add, op1=ALU.mult), r=(f"TM{q + 2}", f"BIG{gt}"), w=(f"TM{q}",))
            P.add('act', lambda e, Y=Y, A_=A_: e.activation(out=r_(Y), in_=A_[:, :T], func=AF.Copy, scale=0.5), r=(f"TM{q}",), w=(f"BIG{gt}",))
        for m in range(KT):
            pa = proj_fm(w_glu[j], m * 128, KT, "BIG", BIG, T)
            pb = proj_fm(w_glu[j], D + m * 128, KT, "BIG", BIG, T)
            q = m % 2
            P.add('act', lambda e, pb=pb, q=q: e.activation(out=TM[q][:, :T], in_=PS[pb][:, :T], func=AF.Sigmoid), r=(f"ps{pb}",), w=(f"TM{q}",))
            P.add('dve', lambda e, pa=pa, q=q: e.tensor_tensor(out=TM[q + 2][:, :T], in0=PS[pa][:, :T], in1=TM[q][:, :T], op=ALU.mult), r=(f"ps{pa}", f"TM{q}"), w=(f"TM{q + 2}",))
            P.add('dve', lambda e, m=m, q=q: e.tensor_tensor(out=XT[:, m, :T], in0=XT[:, m, :T], in1=TM[q + 2][:, :T], op=ALU.add), r=(f"XT{m}", f"TM{q + 2}"), w=(f"XT{m}",))

    if MIXERS >= 2:
        for j_ in range(2):
            ssm_prep(j_)
        P.add('dve', lambda e: e.tensor_copy(STG[:, 0:1], STG[:, 1:2]), r=("prep",), w=tuple(f"XT{k}" for k in range(KT)) + ("bscr", "cscr", "prep"))

    ccall = tuple(f"CC{li}_{j}" for li in range(DEPTH) for j in range(FT))

    def run_block(kind, t0, T, out0, first=True, last=True):
        src = xT_p if kind == 'p' else xT_s
        P.add('sp', lambda e: e.dma_start(out=XT[:, :, :T], in_=src.rearrange("(k p) t -> p k t", p=128)[:, :, t0:t0 + T]),
              w=tuple(f"XT{k}" for k in range(KT)), dma="xin")
        for li in range(layers):
            if li % 2 == 0 and MIXERS >= 1:
                retention(T, li, kind, first, last, t0)
            if li % 2 == 1 and MIXERS >= 2:
                ssm(T, li, kind, first, last)
            ffn(T, li, kind)
        rmsnorm(T, 8)
        P.add('sp', lambda e: e.dma_start(out=yT.rearrange("(k p) t -> p k t", p=128)[:, :, out0:out0 + T], in_=HT[:, :, :T]),
              r=tuple(f"HT{k}" for k in range(KT)), dma="yout")

    P.add('dve', lambda e: e.memset(CC[:], 0.0), w=ccall)
    for b in range(nblk_p):
        run_block('p', b * TB, TB, b * TB, first=(b == 0), last=(b == nblk_p - 1))
    P.add('sp', lambda e: e.dma_start(out=conv_p, in_=CC[:]), r=ccall, dma="cout")
    P.add('sp', lambda e: e.dma_start(out=CC[:], in_=cache_s), w=ccall, dma="cin")
    run_block('s', 0, DEC_S, nblk_p * TB)
    P.add('sp', lambda e: e.dma_start(out=conv_s, in_=CC[:]), r=ccall, dma="cout2")

    P.emit(nc)
    st.close()
    return nc


def host_inputs(inp, core, nblk_p):
    f = np.float32
    norms = np.concatenate([inp['norm_mix'], inp['norm_ffn'], inp['norm_final'][None]], 0).astype(f)
    norms = np.ascontiguousarray(norms.reshape(9, KT, 128).transpose(2, 0, 1))
    cw = np.ascontiguousarray(np.asarray(inp['ffn_conv_w'], f).reshape(DEPTH, 3, FT, 128).transpose(3, 0, 1, 2))
    cb = np.ascontiguousarray(np.asarray(inp['ffn_conv_b'], f).reshape(DEPTH, FT, 128).transpose(2, 0, 1))
    cs = np.ascontiguousarray(np.asarray(inp['cache_conv'], f)[:, core].reshape(DEPTH, 2, FT, 128).transpose(3, 0, 2, 1))
    return {
        "xT_p": np.ascontiguousarray(np.asarray(inp['x_prompt'], f)[0].T),
        "xT_s": np.ascontiguousarray(np.asarray(inp['x_sample'], f)[core].T),
        "norms": norms,
        **pm_weights(inp),
        "conv_w": cw, "conv_b": cb, "cache_s": cs,
        "ones_d": np.ones((128, 128), f),
        "ident_d": np.eye(128, dtype=f),
        "gn_d": np.ascontiguousarray(np.asarray(inp['ret_gn'], f).reshape(2, FT, 128).transpose(2, 0, 1)),
        "sret_d": np.ascontiguousarray(np.asarray(inp['state_ret'], f)[:, core]),
        **rot_tables(),
        **ssm_host(inp, core),
    }


def ssm_host(inp, core):
    f = np.float32
    G, Pn, C = 128, 64, 16

    def sm(a):
        return a.reshape(64, 2, 64).transpose(1, 2, 0).reshape(128, 64)

    def rc(a):
        return np.broadcast_to(a.reshape(16, 8, 1, 64).transpose(1, 2, 0, 3), (8, 16, 16, 64)).reshape(128, 1024)
    sm_d = np.zeros((128, 2, 3, 64), f)
    rc_d = np.zeros((128, 2, 5, 1024), f)
    csm_d = np.zeros((128, 2, 2, 1024), f)
    h0_d = np.zeros((128, 2, 64, 2), f)
    for j in range(2):
        are, aim = np.asarray(inp['ssm_a_re'][j], f), np.asarray(inp['ssm_a_im'][j], f)
        ldt = np.broadcast_to(np.asarray(inp['ssm_log_dt'][j], f)[:, None], (G, Pn))
        sm_d[:, j, 0], sm_d[:, j, 1], sm_d[:, j, 2] = sm(are), sm(aim), sm(np.ascontiguousarray(ldt))
        rc_d[:, j, 0], rc_d[:, j, 1], rc_d[:, j, 2] = rc(are), rc(aim), rc(np.ascontiguousarray(ldt))
        for i, nm in enumerate(['ssm_b_re', 'ssm_b_im']):
            b = np.asarray(inp[nm][j], f)
            rc_d[:, j, 3 + i] = b.reshape(16, 8, 64, 16).transpose(1, 3, 0, 2).reshape(128, 1024)
        for i, nm in enumerate(['ssm_c_re', 'ssm_c_im']):
            c = np.asarray(inp[nm][j], f)
            csm_d[:, j, i] = c.reshape(64, 2, 16, 64).transpose(1, 3, 0, 2).reshape(128, 1024)
        h0_d[:, j, :, 0] = sm(np.asarray(inp['state_ssm_re'][j, core], f))
        h0_d[:, j, :, 1] = sm(np.asarray(inp['state_ssm_im'][j, core], f))
    gl = np.arange(128) // 16
    maskb = (gl[:, None] == np.arange(8)[None, :]).astype(f)
    ap_ = np.arange(128) // 64
    maska = np.stack([(ap_ == 0), (ap_ == 1), (ap_ == 0) * -1.0, (ap_ == 1) * -1.0], 1).astype(f)
    dsk = np.ascontiguousarray(np.asarray(inp['ssm_d'], f).reshape(2, KT, 128).transpose(2, 0, 1))
    return {"sm_d": sm_d, "rc_d": rc_d, "csm_d": csm_d, "maskb_d": maskb,
            "maska_d": maska, "dsk_d": dsk, "h0_d": h0_d}


_PM = {}


def pm_weights(inp):
    if _PM:
        return _PM
    f = np.float32

    def pm(w, n=128):
        L, K, N = w.shape
        return np.ascontiguousarray(w.reshape(L, K // 128, 128, N // n, n).transpose(0, 3, 2, 1, 4)).reshape(L, N // n, 128, (K // 128) * n)
    def pm2(w):
        L, K, N = w.shape
        return np.ascontiguousarray(w.reshape(L, 2, 16, 128, N // 128, 128).transpose(0, 4, 1, 3, 2, 5)).reshape(L, N // 128, 2, 128, 16 * 128)

    def pmv(w):
        L, K, N = w.shape
        return np.ascontiguousarray(w.reshape(L, 2, 8, 128, N // 256, 256).transpose(0, 4, 1, 3, 2, 5)).reshape(L, N // 256, 2, 128, 8 * 256)
    w_in = np.asarray(inp['ret_w_in'], f)
    _PM.update({
        "w_up": pm(np.asarray(inp['ffn_w_up'], f)),
        "w_down": pm2(np.asarray(inp['ffn_w_down'], f)),
        "w_in": pm(w_in),
        "w_v": pmv(np.ascontiguousarray(w_in[:, :, 4096:8192])),
        "w_out": pm2(np.asarray(inp['ret_w_out'], f)),
        "w_glu": pm(np.asarray(inp['ssm_w_glu'], f)),
    })
    return _PM


_TAB = {}


def rot_tables():
    if _TAB:
        return _TAB
    f = np.float32
    half = DK // 2
    inv = (np.float32(10000.0) ** (-np.arange(half, dtype=f) / f(half))).astype(f)

    def cs(pos):
        ang = (pos.astype(f)[None, :] * inv[:, None]).astype(f)
        return np.cos(ang).astype(f), np.sin(ang).astype(f)
    cp, sp_ = cs(np.arange(SEQ))
    c_s, s_s = cs(PAST + np.arange(DEC_S))
    logg = np.log1p(-np.exp2(-5.0 - np.arange(H, dtype=np.float64)))
    l = np.arange(TB)
    cross = np.exp(((l % 64) + 1.0)[None, :] * logg[:, None])
    m = np.arange(128)
    same = (m[:, None] // 64) == (m[None, :] // 64)
    intra = np.exp(np.abs(m[:, None] - m[None, :])[None] * logg[:, None, None]) * same[None] * DK ** -0.5
    kdec = np.exp((63.0 - (m % 64))[:, None] * logg[None, :]) * DK ** -0.5
    _TAB.update({
        "cosp_d": cp, "sinp_d": sp_, "coss_d": c_s, "sins_d": s_s,
        "cross_d": np.ascontiguousarray(np.broadcast_to(cross[None].astype(f), (128, H, TB))),
        "intra_d": np.ascontiguousarray(intra.transpose(1, 0, 2).astype(f)),
        "kdec_d": np.ascontiguousarray(kdec.astype(f)),
    })
    return _TAB


def kernel(**inp):
    nblk_p = SEQ // TB
    nc = build(nblk_p)
    in_maps = [host_inputs(inp, c, nblk_p) for c in range(NCORES)]
    res = run_bass_kernel_spmd(nc, in_maps, core_ids=list(range(NCORES)))
    R = res.results
    f = np.float32
    NP = nblk_p * TB
    y_p = np.ascontiguousarray(R[0]["yT"][:, :NP].T)[None]
    y_s = np.stack([np.ascontiguousarray(R[c]["yT"][:, NP:].T) for c in range(NCORES)], 0)

    def conv_back(a):
        return np.ascontiguousarray(a.transpose(1, 3, 2, 0).reshape(DEPTH, 2, FF))
    conv_p = conv_back(R[0]["conv_p"])[:, None]
    conv_s = np.stack([conv_back(R[c]["conv_s"]) for c in range(NCORES)], 1)
    def ssm_back(a):
        b = a.reshape(2, 2, 64, 64, 2).transpose(0, 3, 1, 2, 4).reshape(2, 128, 64, 2)
        return np.ascontiguousarray(b[..., 0]), np.ascontiguousarray(b[..., 1])
    rp = R[0]["ret_p"][:, None]
    rs = np.stack([R[c]["ret_s"] for c in range(NCORES)], 1)
    re_p, im_p = ssm_back(R[0]["ssm_p"])
    ss = [ssm_back(R[c]["ssm_s"]) for c in range(NCORES)]
    re_s = np.stack([x[0] for x in ss], 1)
    im_s = np.stack([x[1] for x in ss], 1)
    return (y_p.astype(f), y_s.astype(f), rp.astype(f), rs.astype(f),
            re_p[:, None].astype(f), im_p[:, None].astype(f), re_s.astype(f), im_s.astype(f),
            conv_p.astype(f), conv_s.astype(f))
```
